# Optimizing a Trainium2 kernel written in Bass

```python
import jax, jax.numpy as jnp
from jax import lax
import numpy as np

D_MODEL = 2048
BATCH = 16
SEQ = 2048
DEPTH = 4
DEC_BATCH = 4
DEC_SEQ = 2048
PAST_LEN = 128

N_MIXERS = 2
N_CONV_LAYERS = (DEPTH + 1) // 2
N_ATTN_LAYERS = DEPTH // 2
CONV_WIDTH = 3
HEAD_DIM = 128
N_HEADS = D_MODEL // HEAD_DIM
N_KV_HEADS = 4
GROUP = N_HEADS // N_KV_HEADS
WINDOW = 128
BLOCK = 128
NEIGH = WINDOW // BLOCK
BAND = (2 * NEIGH + 1) * BLOCK
ROPE_THETA = 10000.0
D_FF = ((8 * D_MODEL + 3 * 256 - 1) // (3 * 256)) * 256
EPS = 1e-6
NEG_INF = -1e30

kernel_name = "hybrid_shortconv_swa_sink_encoder"


def rmsnorm(x, g):
    xf = x.astype(jnp.float32)
    r = lax.rsqrt(jnp.mean(xf * xf, axis=-1, keepdims=True) + EPS)
    return (xf * r).astype(x.dtype) * g


def short_conv_mixer(x, w_in, w_conv, w_out):
    bch = x @ w_in
    b, c, h = jnp.split(bch, 3, axis=-1)
    u = c * h
    up = jnp.pad(u, ((0, 0), (CONV_WIDTH // 2, CONV_WIDTH // 2), (0, 0)))
    S = x.shape[1]
    v = sum(w_conv[t] * up[:, t:t + S] for t in range(CONV_WIDTH))
    return (b * v) @ w_out


def rope(x, cos, sin):
    x1, x2 = jnp.split(x, 2, axis=-1)
    return jnp.concatenate([x1 * cos - x2 * sin, x2 * cos + x1 * sin], axis=-1)


def band_mask(S):
    nb = S // BLOCK
    qpos = np.arange(nb)[:, None, None] * BLOCK + np.arange(BLOCK)[None, :, None]
    kpos = (np.arange(nb)[:, None, None] - NEIGH) * BLOCK + np.arange(BAND)[None, None, :]
    valid = (np.abs(qpos - kpos) <= WINDOW) & (kpos >= 0) & (kpos < S)
    return jnp.asarray(valid)


def windowed_gqa_sink(x, w_qkv, w_o, sink):
    B, S, _ = x.shape
    nb = S // BLOCK
    qkv = x @ w_qkv
    q, k, v = jnp.split(qkv, [N_HEADS * HEAD_DIM, (N_HEADS + N_KV_HEADS) * HEAD_DIM], axis=-1)
    q = q.reshape(B, S, N_HEADS, HEAD_DIM)
    k = k.reshape(B, S, N_KV_HEADS, HEAD_DIM)
    v = v.reshape(B, S, N_KV_HEADS, HEAD_DIM)

    inv_freq = 1.0 / (ROPE_THETA ** (jnp.arange(0, HEAD_DIM, 2, dtype=jnp.float32) / HEAD_DIM))
    ang = jnp.arange(S, dtype=jnp.float32)[:, None] * inv_freq[None, :]
    cos = jnp.cos(ang)[:, None, :].astype(x.dtype)
    sin = jnp.sin(ang)[:, None, :].astype(x.dtype)
    q = rope(q, cos, sin)
    k = rope(k, cos, sin)

    q = q.reshape(B, nb, BLOCK, N_KV_HEADS, GROUP, HEAD_DIM)
    pad = ((0, 0), (NEIGH * BLOCK, NEIGH * BLOCK), (0, 0), (0, 0))
    kp = jnp.pad(k, pad).reshape(B, nb + 2 * NEIGH, BLOCK, N_KV_HEADS, HEAD_DIM)
    vp = jnp.pad(v, pad).reshape(B, nb + 2 * NEIGH, BLOCK, N_KV_HEADS, HEAD_DIM)
    kb = jnp.concatenate([kp[:, t:t + nb] for t in range(2 * NEIGH + 1)], axis=2)
    vb = jnp.concatenate([vp[:, t:t + nb] for t in range(2 * NEIGH + 1)], axis=2)

    scale = HEAD_DIM ** -0.5
    s = jnp.einsum('bnqkgd,bnpkd->bnkgqp', q, kb).astype(jnp.float32) * scale
    mask = band_mask(S)[None, :, None, None]
    s = jnp.where(mask, s, NEG_INF)
    sink_b = jnp.broadcast_to(sink.astype(jnp.float32).reshape(N_KV_HEADS, GROUP)[None, None, :, :, None, None],
                              s.shape[:-1] + (1,))
    p = jax.nn.softmax(jnp.concatenate([s, sink_b], axis=-1), axis=-1)[..., :BAND].astype(v.dtype)
    o = jnp.einsum('bnkgqp,bnpkd->bnqkgd', p, vb).reshape(B, S, N_HEADS * HEAD_DIM)
    return o @ w_o


def swiglu(x, w_gate, w_up, w_down):
    return (jax.nn.silu(x @ w_gate) * (x @ w_up)) @ w_down


def trunk(x, conv_w_in, conv_w_dw, conv_w_out, attn_w_qkv, attn_w_o, attn_sink,
          ffn_w_gate, ffn_w_up, ffn_w_down, g_mix_pre, g_mix_post, g_ffn_pre, g_ffn_post):
    for i in range(DEPTH):
        h = rmsnorm(x, g_mix_pre[i])
        j = i // N_MIXERS
        if i % N_MIXERS == 0:
            h = short_conv_mixer(h, conv_w_in[j], conv_w_dw[j], conv_w_out[j])
        else:
            h = windowed_gqa_sink(h, attn_w_qkv[j], attn_w_o[j], attn_sink[j])
        x = x + rmsnorm(h, g_mix_post[i])
        h = swiglu(rmsnorm(x, g_ffn_pre[i]), ffn_w_gate[i], ffn_w_up[i], ffn_w_down[i])
        x = x + rmsnorm(h, g_ffn_post[i])
    return x


def setup_inputs(seed: int = 0) -> dict:
    key = jax.random.key(seed)
    ks = jax.random.split(key, 16)
    D, F = D_MODEL, D_FF
    QKV = (N_HEADS + 2 * N_KV_HEADS) * HEAD_DIM
    nrm = jax.random.normal
    f32 = jnp.float32
    return {
        "x_prompt": nrm(ks[0], (BATCH, SEQ, D), f32),
        "x_sample": nrm(ks[1], (DEC_BATCH, DEC_SEQ, D), f32),
        "conv_w_in": nrm(ks[2], (N_CONV_LAYERS, D, 3 * D), f32) * D ** -0.5,
        "conv_w_dw": nrm(ks[3], (N_CONV_LAYERS, CONV_WIDTH, D), f32) * CONV_WIDTH ** -0.5,
        "conv_w_out": nrm(ks[4], (N_CONV_LAYERS, D, D), f32) * D ** -0.5,
        "attn_w_qkv": nrm(ks[5], (N_ATTN_LAYERS, D, QKV), f32) * D ** -0.5,
        "attn_w_o": nrm(ks[6], (N_ATTN_LAYERS, N_HEADS * HEAD_DIM, D), f32) * (N_HEADS * HEAD_DIM) ** -0.5,
        "attn_sink": nrm(ks[7], (N_ATTN_LAYERS, N_HEADS), f32) * 0.5,
        "ffn_w_gate": nrm(ks[8], (DEPTH, D, F), f32) * D ** -0.5,
        "ffn_w_up": nrm(ks[9], (DEPTH, D, F), f32) * D ** -0.5,
        "ffn_w_down": nrm(ks[10], (DEPTH, F, D), f32) * F ** -0.5,
        "g_mix_pre": 1.0 + 0.05 * nrm(ks[11], (DEPTH, D), f32),
        "g_mix_post": 1.0 + 0.05 * nrm(ks[12], (DEPTH, D), f32),
        "g_ffn_pre": 1.0 + 0.05 * nrm(ks[13], (DEPTH, D), f32),
        "g_ffn_post": 1.0 + 0.05 * nrm(ks[14], (DEPTH, D), f32),
    }


def reference(x_prompt, x_sample, conv_w_in, conv_w_dw, conv_w_out, attn_w_qkv, attn_w_o,
              attn_sink, ffn_w_gate, ffn_w_up, ffn_w_down, g_mix_pre, g_mix_post, g_ffn_pre,
              g_ffn_post):
    y_prompt = trunk(x_prompt, conv_w_in, conv_w_dw, conv_w_out, attn_w_qkv, attn_w_o, attn_sink,
                     ffn_w_gate, ffn_w_up, ffn_w_down, g_mix_pre, g_mix_post, g_ffn_pre, g_ffn_post)
    y_sample = trunk(x_sample, conv_w_in, conv_w_dw, conv_w_out, attn_w_qkv, attn_w_o, attn_sink,
                     ffn_w_gate, ffn_w_up, ffn_w_down, g_mix_pre, g_mix_post, g_ffn_pre, g_ffn_post)
    return (y_prompt, y_sample)
```

```python
import math
from contextlib import ExitStack

import numpy as np
import concourse.bass as bass
import concourse.mybir as mybir
from concourse.bass_utils import run_bass_kernel_spmd

F32 = mybir.dt.float32
BF16 = mybir.dt.bfloat16
ALU = mybir.AluOpType
AF = mybir.ActivationFunctionType

D = 2048
NCH = 16
DFF = 5632
NF = 44
NT = 1408
NREAL = 1024
NUNITS = 5
NCORES = 8
SEQ = 2048
EPS = 1e-6
NH = 16
NKV = 4
QKV = 3072
SCALE = 128 ** -0.5
CELL = 512
NSLOT = 3
SLOT_E = 4096
NSCR = 408

CONV_U = {0: [(0, 428), (428, 427), (855, 427)], 2: [(0, 385), (385, 384), (769, 384)]}
CONV_Z = {0: [(0, 428), (428, 427), (855, 426)], 2: [(0, 385), (385, 384), (769, 383)]}
ATT_KV = {1: [(0, 512), (512, 512), (1024, 384)], 3: [(0, 384), (384, 384), (768, 384)]}
ATT_Q = {1: [(0, 512), (512, 384), (896, 257)], 3: [(0, 512), (512, 512)]}
FFN_T = {0: [(0, 427), (427, 427), (854, 427)], 1: [(0, 385), (385, 384), (769, 384)],
         2: [(0, 384), (384, 384), (768, 384)], 3: [(0, 512), (512, 512)]}


class V:
    __slots__ = ("ap", "tid", "c0", "c1")

    def __init__(self, ap, tid, c0=None, c1=None):
        self.ap = ap
        self.tid = tid
        if c0 is None:
            pairs = ap.ap
            row = pairs[0][0]
            off = int(ap.offset) % row if row > 0 else int(ap.offset)
            ext = sum((cnt - 1) * st for st, cnt in pairs[1:])
            es = 4 if ap.dtype == F32 else 2
            c0 = (off * es) // CELL
            c1 = ((off + ext + 1) * es + CELL - 1) // CELL
        self.c0 = c0
        self.c1 = c1


class Prog:
    ENGS = ("pe", "act", "dve", "pool", "sp")

    def __init__(self):
        self.ops = []
        self.lastw = {}
        self.readers = {}

    def op(self, eng, fn, reads=(), writes=(), dma_sem=None, group=None):
        idx = len(self.ops)
        deps = set()
        lastw, readers = self.lastw, self.readers
        for v in reads:
            for c in range(v.c0, v.c1):
                w = lastw.get((v.tid, c))
                if w is not None:
                    deps.add(w)
                if v.tid == "ps":
                    r = readers.get((v.tid, c))
                    if r:
                        deps.update(i_ for k_, i_ in r.items() if k_ != eng)
        for v in writes:
            for c in range(v.c0, v.c1):
                k = (v.tid, c)
                w = lastw.get(k)
                if w is not None:
                    deps.add(w)
                r = readers.get(k)
                if r:
                    deps.update(r.values())
        rkey = eng if dma_sem is None else ("dma", idx)
        for v in writes:
            for c in range(v.c0, v.c1):
                k = (v.tid, c)
                lastw[k] = idx
                readers[k] = {}
        for v in reads:
            for c in range(v.c0, v.c1):
                k = (v.tid, c)
                r = readers.get(k)
                if r is None:
                    r = readers[k] = {}
                r[rkey] = idx
        deps.discard(idx)
        self.ops.append([eng, fn, deps, dma_sem, False, group if group is not None else idx])
        return idx

    def emit(self, nc, es):
        ops = self.ops
        eng_sem = {e: es.enter_context(nc.semaphore("c_" + e)) for e in self.ENGS}
        def need(op, d):
            dop = ops[d]
            if dop[3] is not None:
                return True
            if dop[0] == op[0] and op[0] == "pe" and op[3] is None:
                return False
            return True
        for op in ops:
            for d in op[2]:
                if need(op, d):
                    ops[d][4] = True
        cnt = {e: 0 for e in self.ENGS}
        dcnt = {}
        token = [None] * len(ops)
        gend = {}
        for i, op in enumerate(ops):
            if op[3] is not None:
                s = op[3]
                dcnt[s] = dcnt.get(s, 0) + 16
                gend[(id(s), op[5])] = dcnt[s]
        for i, op in enumerate(ops):
            if op[3] is not None:
                s = op[3]
                token[i] = (s, gend[(id(s), op[5])])
            elif op[4]:
                cnt[op[0]] += 1
                token[i] = (eng_sem[op[0]], cnt[op[0]])
        per_eng = {e: [] for e in self.ENGS}
        for i, op in enumerate(ops):
            per_eng[op[0]].append(i)
        final_dma = dict(dcnt)

        def run(ename, eh, finals=()):
            waited = {}
            for i in per_eng[ename]:
                op = ops[i]
                w = {}
                for d in op[2]:
                    if need(op, d):
                        s, val = token[d]
                        if val > w.get(s, 0):
                            w[s] = val
                for s, val in w.items():
                    if val > waited.get(s, 0):
                        eh.wait_ge(s, val)
                        waited[s] = val
                ins = op[1](eh)
                if op[3] is not None:
                    ins.then_inc(op[3], 16)
                elif op[4]:
                    ins.then_inc(eng_sem[ename], 1)
            for s in finals:
                if s in final_dma:
                    eh.wait_ge(s, final_dma[s])
        return run


def build_nc():
    nc = bass.Bass("TRN2", target_bir_lowering=False)
    dt_ = nc.dram_tensor
    xd = dt_("x_units", [NUNITS, D, NT], F32, kind="ExternalInput").ap()
    csd = dt_("cs_tab", [NUNITS, 2, 128, NT], F32, kind="ExternalInput").ap()
    cwd = dt_("cw_tab", [NUNITS, 128, 96], F32, kind="ExternalInput").ap()
    gvd = dt_("gv_tab", [128, 256], F32, kind="ExternalInput").ap()
    skd = dt_("sink_tab", [128, 32], F32, kind="ExternalInput").ap()
    w_in_d = dt_("conv_w_in", [2, D, 3 * D], F32, kind="ExternalInput").ap()
    w_out_d = dt_("conv_w_out", [2, D, D], F32, kind="ExternalInput").ap()
    w_qkv_d = dt_("attn_w_qkv", [2, D, QKV], F32, kind="ExternalInput").ap()
    w_o_d = dt_("attn_w_o", [2, D, D], F32, kind="ExternalInput").ap()
    w_g_d = dt_("ffn_w_gate", [4, D, DFF], F32, kind="ExternalInput").ap()
    w_u_d = dt_("ffn_w_up", [4, D, DFF], F32, kind="ExternalInput").ap()
    w_d_d = dt_("ffn_w_down", [4, DFF, D], F32, kind="ExternalInput").ap()
    yd = dt_("y_units", [NUNITS, D, NREAL], F32, kind="ExternalOutput").ap()
    scr_parts = [dt_("wscr%d" % i, [102, 128, SLOT_E], BF16, kind="Internal").ap() for i in range(4)]

    es = ExitStack()
    with es:
        sb = lambda name, shape, dt: es.enter_context(nc.sbuf_tensor(name, shape, dt))
        xbuf = sb("xbuf", [128, NCH, NT], F32)
        shr = sb("shr", [128, 35840], BF16)
        ring = sb("ring", [128, NSLOT, SLOT_E], BF16)
        gv = sb("gv", [128, 256], F32)
        cw = sb("cw", [128, 96], F32)
        sink = sb("sink", [128, 32], F32)
        esink = sb("esink", [128, 32], F32)
        ones_bf = sb("ones_bf", [128, 128], BF16)
        mask1 = sb("mask1", [128, 128], BF16)
        mask2 = sb("mask2", [128, 128], BF16)
        rt = sb("rt", [128, 2, 512], F32)
        sq = sb("sq", [128, 4, 512], BF16)
        tmp = sb("tmp", [128, 6, 600], F32)
        small = sb("small", [128, 8, 16], F32)
        ps = es.enter_context(nc.psum_tensor("ps", [128, 8, 512], F32))
        sem = lambda n: es.enter_context(nc.semaphore(n))
        slot_sem = [sem("slot%d" % i) for i in range(NSLOT)]
        scrst_sem = [sem("scrst%d" % i) for i in range(NSLOT)]
        scr_idx = {}
        cs_sem, cw_sem, const_sem = sem("cs"), sem("cwl"), sem("cst")
        xload_sem = [sem("xl%d" % i) for i in range(4)]
        ystore_sem = [sem("ys%d" % i) for i in range(4)]

        P = Prog()

        def X(c, t0, n):
            return V(xbuf[:, c, t0:t0 + n], "xbuf")

        def shr_bf(e0, a, b):
            return shr[:, e0:e0 + a * b].rearrange("p (a b) -> p a b", a=a)

        def shr_f32(e0, a, b):
            return shr[:, e0:e0 + 2 * a * b].bitcast(F32).rearrange("p (a b) -> p a b", a=a)

        K = 512
        H_ = shr_bf(0, 16, 512)
        def Hv(c, n, t0=0):
            return V(H_[:, c, t0:t0 + n], "shr")
        ACT_ = shr_bf(16 * K, 22, 512)
        YS_ffn = shr_f32(38 * K, 16, 512)
        YS_low = shr_f32(0, 16, 512)
        Z_ = [shr_bf(38 * K, 16, 512), shr_bf(54 * K, 16, 512)]
        Q_ = shr_bf(16 * K, 16, 512)
        O_ = shr_bf(32 * K, 16, 512)
        COS_ = shr_f32(32 * K, 1, 512)
        SIN_ = shr_f32(34 * K, 1, 512)
        PT_ = shr_bf(0, 6, 512)
        KT_ = shr[:, 48 * K:48 * K + 4 * NT].rearrange("p (a b) -> p a b", a=4)
        VV_ = shr_bf(48 * K + 4 * NT, 11, 512)

        def PS(b, n=512, p0=0, p1=128):
            return V(ps[p0:p1, b, 0:n], "ps", b, b + 1)

        def TMP(i, n, o=0):
            return V(tmp[:, i, o:o + n], "tmp")

        bank_rr = [0]
        nbanks = [6]
        def nbank():
            b = bank_rr[0] % nbanks[0]
            bank_rr[0] = (b + 1) % nbanks[0]
            return b
        stat_rr = [0]
        def sbank():
            b = 6 + stat_rr[0]
            stat_rr[0] ^= 1
            return b
        slot_rr = [0]
        rt_rr = [0]
        sq_rr = [0]

        def GV(kind, layer, c):
            i = (kind * 4 + layer) * 16 + c
            return gv[:, i:i + 1]

        C_ONES = V(ones_bf[:, :], "ones")
        C_M1 = V(mask1[:, :], "mask1")
        C_M2 = V(mask2[:, :], "mask2")
        C_GV = V(gv[:, :], "gv")
        C_ES = V(esink[:, :], "esink")
        C_CW = V(cw[:, :], "cw")

        def load_slot(src_ap, k, ncols):
            s = slot_rr[0]
            slot_rr[0] = (s + 1) % NSLOT
            ne = k * ncols
            view = ring[:, s, 0:ne].rearrange("p (k c) -> p k c", k=k)
            vv = V(view, "ring")
            key = (src_ap.tensor.name, int(src_ap.offset), tuple(src_ap.ap))
            if key not in scr_idx:
                si = scr_idx[key] = len(scr_idx)
                assert si < NSCR
                P.op("pool", lambda e, o=view, i=src_ap: e.dma_start(out=o, in_=i), writes=[vv], dma_sem=slot_sem[s])
                P.op("sp", lambda e, o=scr_parts[si // 102][si % 102, :, 0:ne], i=ring[:, s, 0:ne]: e.dma_start(out=o, in_=i),
                     reads=[vv], writes=[V(None, "scr", si, si + 1)], dma_sem=scrst_sem[s])
            else:
                si = scr_idx[key]
                P.op("sp", lambda e, o=ring[:, s, 0:ne], i=scr_parts[si // 102][si % 102, :, 0:ne]: e.dma_start(out=o, in_=i),
                     reads=[V(None, "scr", si, si + 1)], writes=[vv], dma_sem=slot_sem[s])
            return view

        def wview(w2d, c0, ncols, r0=0, k=16):
            return w2d.rearrange("(k p) c -> p k c", p=128)[:, r0:r0 + k, c0:c0 + ncols]

        def mm_group(bank, n, lhs_list, rhs_list, reads, p1=128):
            out = PS(bank, n, 0, p1)
            L = len(lhs_list)
            for i in range(L):
                P.op("pe", lambda e, o=out.ap, l=lhs_list[i], r=rhs_list[i], st=(i == 0), sp=(i == L - 1):
                     e.matmul(o, l, r, start=st, stop=sp), reads=reads if i == 0 else reads, writes=[out])
            return out

        def stats_begin():
            return sbank()

        def stats_sq(n, src_v):
            s = sq_rr[0]
            sq_rr[0] = (s + 1) % 4
            sv = V(sq[:, s, 0:n], "sq")
            P.op("act", lambda e, o=sv.ap, i=src_v.ap: e.activation(out=o, in_=i, func=AF.Square), reads=[src_v], writes=[sv])
            return sv

        def stats_mm(bank, n, sv, first, last):
            out = PS(bank, n)
            P.op("pe", lambda e, o=out.ap, r=sv.ap, st=first, sp=last: e.matmul(o, ones_bf[:, :], r, start=st, stop=sp),
                 reads=[sv, C_ONES], writes=[out])

        def stats_add(bank, n, src_v, first, last):
            stats_mm(bank, n, stats_sq(n, src_v), first, last)

        def stats_finish(bank, n):
            i = rt_rr[0]
            rt_rr[0] ^= 1
            rv = V(rt[:, i, 0:n], "rt")
            src = PS(bank, n)
            P.op("act", lambda e, o=rv.ap, s_=src.ap: e.activation(out=o, in_=s_, func=AF.Sqrt, scale=1.0 / D, bias=EPS),
                 reads=[src], writes=[rv])
            P.op("dve", lambda e, o=rv.ap: e.reciprocal(out=o, in_=o), reads=[rv], writes=[rv])
            return rv

        pre_r = {}

        def norm_stats(kind, layer, t0, n):
            bank = stats_begin()
            for c in range(NCH):
                stats_add(bank, n, X(c, t0, n), c == 0, c == NCH - 1)
            pre_r[(kind, layer, t0, n)] = stats_finish(bank, n)

        pre_h = set()

        def norm_h(kind, layer, t0, n):
            if (kind, layer, t0, n) in pre_h:
                pre_h.discard((kind, layer, t0, n))
                return
            if (kind, layer, t0, n) not in pre_r:
                norm_stats(kind, layer, t0, n)
            rv = pre_r.pop((kind, layer, t0, n))
            for c in range(NCH):
                hv = Hv(c, n)
                xv = X(c, t0, n)
                P.op("dve", lambda e, o=hv.ap, a=xv.ap, g=GV(kind, layer, c), r=rv.ap:
                     e.scalar_tensor_tensor(out=o, in0=a, scalar=g, in1=r, op0=ALU.mult, op1=ALU.mult),
                     reads=[xv, rv, C_GV], writes=[hv])

        def post_norm_update(kind, layer, t0, n, ys3, bank, nxt=None, h_next=False):
            if nxt is not None:
                norm_stats(*nxt)
            rv = stats_finish(bank, n)

            def upd(m):
                yv = V(ys3[:, m, 0:n], "shr")
                xv = X(m, t0, n)
                P.op("dve" if m < 8 else "pool", lambda e, o=yv.ap, r=rv.ap: e.tensor_tensor(out=o, in0=o, in1=r, op=ALU.mult),
                     reads=[yv, rv], writes=[yv])
                P.op("dve", lambda e, o=xv.ap, y=yv.ap, g=GV(kind, layer, m):
                     e.scalar_tensor_tensor(out=o, in0=y, scalar=g, in1=o, op0=ALU.mult, op1=ALU.add),
                     reads=[yv, xv, C_GV], writes=[xv])
            for m in range(8):
                upd(m)
            if nxt is not None and h_next:
                norm_h(*nxt)
                pre_h.add(nxt)
            for m in range(8, NCH):
                upd(m)

        def out_proj(w2d, rhs3, n, ys3, kchunks=16):
            bank_s = stats_begin()
            pend = []
            for mp in range(8):
                wv = load_slot(wview(w2d, mp * 256, 256), 16, 256)
                for jj in range(2):
                    m = mp * 2 + jj
                    b = nbank()
                    lhs = [wv[:, k, jj * 128:(jj + 1) * 128] for k in range(16)]
                    rhs = [rhs3[:, k, 0:n] for k in range(16)]
                    rd = [V(wv, "ring")] + [V(rhs3[:, k, 0:n], "shr") for k in range(16)]
                    out = mm_group(b, n, lhs, rhs, rd)
                    yv = V(ys3[:, m, 0:n], "shr")
                    P.op("act", lambda e, o=yv.ap, i=out.ap: e.activation(out=o, in_=i, func=AF.Copy), reads=[out], writes=[yv])
                    pend.append((stats_sq(n, yv), m))
                    if len(pend) > 2:
                        sv0, m0 = pend.pop(0)
                        stats_mm(bank_s, n, sv0, m0 == 0, m0 == 15)
            for sv0, m0 in pend:
                stats_mm(bank_s, n, sv0, m0 == 0, m0 == 15)
            return bank_s

        def ffn(layer):
            wg, wu, wd = w_g_d[layer], w_u_d[layer], w_d_d[layer]
            tiles = FFN_T[layer]
            norm_h(2, layer, tiles[0][0], tiles[0][1])
            for ti, (t0, n) in enumerate(tiles):
                bank_s = None
                for half in range(2):
                    for fp in range(11):
                        f0 = half * 22 + fp * 2
                        gvw = load_slot(wview(wg, f0 * 128, 256), 16, 256)
                        uvw = load_slot(wview(wu, f0 * 128, 256), 16, 256)
                        for jj in range(2):
                            fl = fp * 2 + jj
                            rhs = [H_[:, k, 0:n] for k in range(16)]
                            hr = [Hv(k, n) for k in range(16)]
                            bg = nbank()
                            og = mm_group(bg, n, [gvw[:, k, jj * 128:(jj + 1) * 128] for k in range(16)], rhs, [V(gvw, "ring")] + hr)
                            bu = nbank()
                            ou = mm_group(bu, n, [uvw[:, k, jj * 128:(jj + 1) * 128] for k in range(16)], rhs, [V(uvw, "ring")] + hr)
                            tv = TMP(fl % 2, n)
                            P.op("act", lambda e, o=tv.ap, i=og.ap: e.activation(out=o, in_=i, func=AF.Silu), reads=[og], writes=[tv])
                            av = V(ACT_[:, fl, 0:n], "shr")
                            P.op("dve", lambda e, o=av.ap, a=ou.ap, b_=tv.ap: e.tensor_tensor(out=o, in0=a, in1=b_, op=ALU.mult),
                                 reads=[ou, tv], writes=[av])
                    if half == 1:
                        bank_s = stats_begin()
                        pend = []
                    for mp in range(8):
                        if half == 1 and mp == 0:
                            if ti + 1 < len(tiles):
                                norm_h(2, layer, tiles[ti + 1][0], tiles[ti + 1][1])
                            elif layer < 3:
                                ft = (CONV_U if (layer + 1) % 2 == 0 else ATT_KV)[layer + 1][0]
                                key = (0, layer + 1, ft[0], ft[1])
                                norm_h(*key)
                                pre_h.add(key)
                        banks = [nbank(), nbank()]
                        outs = [None, None]
                        for kq in range(2):
                            wv = load_slot(wd.rearrange("(f p) c -> p f c", p=128)[:, half * 22 + kq * 11: half * 22 + kq * 11 + 11, mp * 256:mp * 256 + 256], 11, 256)
                            for jj in range(2):
                                out = PS(banks[jj], n)
                                for kk in range(11):
                                    fl = kq * 11 + kk
                                    av = V(ACT_[:, fl, 0:n], "shr")
                                    P.op("pe", lambda e, o=out.ap, l=wv[:, kk, jj * 128:(jj + 1) * 128], r=av.ap, st=(kq == 0 and kk == 0), sp=(kq == 1 and kk == 10):
                                         e.matmul(o, l, r, start=st, stop=sp), reads=[V(wv, "ring"), av], writes=[out])
                                outs[jj] = out
                        for jj in range(2):
                            m = mp * 2 + jj
                            yv = V(YS_ffn[:, m, 0:n], "shr")
                            if half == 0:
                                P.op("act", lambda e, o=yv.ap, i=outs[jj].ap: e.activation(out=o, in_=i, func=AF.Copy), reads=[outs[jj]], writes=[yv])
                            else:
                                P.op("dve", lambda e, o=yv.ap, i=outs[jj].ap: e.tensor_tensor(out=o, in0=i, in1=o, op=ALU.add), reads=[outs[jj], yv], writes=[yv])
                                pend.append((stats_sq(n, yv), m))
                                if len(pend) > 2:
                                    sv0, m0 = pend.pop(0)
                                    stats_mm(bank_s, n, sv0, m0 == 0, m0 == 15)
                    if half == 1:
                        for sv0, m0 in pend:
                            stats_mm(bank_s, n, sv0, m0 == 0, m0 == 15)
                post_norm_update(3, layer, t0, n, YS_ffn, bank_s)

        def conv_mixer(layer):
            j = layer // 2
            win, wout = w_in_d[j], w_out_d[j]
            ut, zt = CONV_U[layer], CONV_Z[layer]
            nt_ = len(ut)
            def CW(t, c):
                i = (j * 3 + t) * 16 + c
                return cw[:, i:i + 1]
            w2arr = cw[:, (j * 3 + 2) * 16:(j * 3 + 2) * 16 + 16]
            def SM(i, c=None):
                if c is None:
                    return V(small[:, i, :], "small")
                return V(small[:, i, c:c + 1], "small")

            def U(ti):
                t0, nu = ut[ti]
                nz = zt[ti][1]
                last = (ti == nt_ - 1)
                par = ti % 2
                norm_h(0, layer, t0, nu)
                hr = [Hv(k, nu) for k in range(16)]
                rhs = [H_[:, k, 0:nu] for k in range(16)]
                for mp in range(8):
                    wb = load_slot(wview(win, mp * 256, 256), 16, 256)
                    wc = load_slot(wview(win, D + mp * 256, 256), 16, 256)
                    wh = load_slot(wview(win, 2 * D + mp * 256, 256), 16, 256)
                    for jj in range(2):
                        m = mp * 2 + jj
                        sl = slice(jj * 128, (jj + 1) * 128)
                        ob = mm_group(nbank(), nu, [wb[:, k, sl] for k in range(16)], rhs, [V(wb, "ring")] + hr)
                        oc = mm_group(nbank(), nu, [wc[:, k, sl] for k in range(16)], rhs, [V(wc, "ring")] + hr)
                        oh = mm_group(nbank(), nu, [wh[:, k, sl] for k in range(16)], rhs, [V(wh, "ring")] + hr)
                        pb = (m % 2) * 3
                        ub = TMP(pb, nu + 2)
                        t1 = TMP(pb + 1, nz)
                        h3 = TMP(pb + 2, nu)
                        P.op("act", lambda e, o=h3.ap, i=oh.ap: e.activation(out=o, in_=i, func=AF.Copy), reads=[oh], writes=[h3])
                        uin = V(tmp[:, pb, 1:nu + 1], "tmp")
                        P.op("dve", lambda e, o=uin.ap, a=oc.ap, b_=h3.ap: e.tensor_tensor(out=o, in0=a, in1=b_, op=ALU.mult),
                             reads=[oc, h3], writes=[uin])
                        u0 = V(tmp[:, pb, 0:1], "tmp")
                        if ti == 0:
                            P.op("dve", lambda e, o=u0.ap: e.memset(o, 0.0), writes=[u0])
                        else:
                            cv = SM(1 - par, m)
                            P.op("dve", lambda e, o=u0.ap, i=cv.ap: e.tensor_copy(out=o, in_=i), reads=[cv], writes=[u0])
                        if not last:
                            P.op("dve", lambda e, o=SM(par, m).ap, i=tmp[:, pb, nu:nu + 1]: e.tensor_copy(out=o, in_=i),
                                 reads=[uin], writes=[SM(par, m)])
                        if ti > 0:
                            P.op("dve", lambda e, o=SM(6, m).ap, i=tmp[:, pb, 1:2]: e.tensor_copy(out=o, in_=i),
                                 reads=[uin], writes=[SM(6, m)])
                        P.op("act", lambda e, o=t1.ap, i=tmp[:, pb, 1:nz + 1], s_=CW(1, m): e.activation(out=o, in_=i, func=AF.Copy, scale=s_),
                             reads=[uin, C_CW], writes=[t1])
                        P.op("dve", lambda e, o=t1.ap, i=tmp[:, pb, 0:nz], s_=CW(0, m):
                             e.scalar_tensor_tensor(out=o, in0=i, scalar=s_, in1=o, op0=ALU.mult, op1=ALU.add),
                             reads=[ub, t1, C_CW], writes=[t1])
                        nw2 = min(nz, nu - 1)
                        t1w = V(tmp[:, pb + 1, 0:nw2], "tmp")
                        P.op("dve", lambda e, o=t1w.ap, i=tmp[:, pb, 2:nw2 + 2], s_=CW(2, m):
                             e.scalar_tensor_tensor(out=o, in0=i, scalar=s_, in1=o, op0=ALU.mult, op1=ALU.add),
                             reads=[ub, t1w, C_CW], writes=[t1w])
                        zv = V(Z_[par][:, m, 0:nz], "shr")
                        P.op("dve", lambda e, o=zv.ap, a=ps[:, ob.c0, 0:nz], b_=t1.ap: e.tensor_tensor(out=o, in0=a, in1=b_, op=ALU.mult),
                             reads=[ob, t1], writes=[zv])
                        if not last:
                            P.op("dve", lambda e, o=SM(2 + par, m).ap, i=ps[:, ob.c0, nz - 1:nz]: e.tensor_copy(out=o, in_=i),
                                 reads=[ob], writes=[SM(2 + par, m)])
                            P.op("dve", lambda e, o=SM(4 + par, m).ap, i=tmp[:, pb + 1, nz - 1:nz]: e.tensor_copy(out=o, in_=i),
                                 reads=[t1], writes=[SM(4 + par, m)])

            def fix(ti):
                par = ti % 2
                nz = zt[ti][1]
                sc = SM(7)
                P.op("dve", lambda e, o=sc.ap, a=SM(6).ap: e.tensor_tensor(out=o, in0=a, in1=w2arr, op=ALU.mult), reads=[SM(6), C_CW], writes=[sc])
                P.op("dve", lambda e, o=sc.ap, a=SM(4 + par).ap: e.tensor_tensor(out=o, in0=o, in1=a, op=ALU.add), reads=[sc, SM(4 + par)], writes=[sc])
                zl = V(Z_[par][:, :, nz - 1], "shr")
                P.op("dve", lambda e, o=zl.ap, a=sc.ap, b_=SM(2 + par).ap: e.tensor_tensor(out=o, in0=a, in1=b_, op=ALU.mult),
                     reads=[sc, SM(2 + par)], writes=[zl])

            def Z(ti):
                t0, nz = zt[ti]
                bank_s = out_proj(wout, Z_[ti % 2], nz, YS_low)
                if ti + 2 < nt_:
                    nxt, hn = (0, layer, ut[ti + 2][0], ut[ti + 2][1]), True
                elif ti == nt_ - 1:
                    nxt, hn = (2, layer, FFN_T[layer][0][0], FFN_T[layer][0][1]), True
                else:
                    nxt, hn = None, False
                post_norm_update(1, layer, t0, nz, YS_low, bank_s, nxt, hn)

            U(0)
            for ti in range(1, nt_):
                U(ti)
                fix(ti - 1)
                Z(ti - 1)
            Z(nt_ - 1)

        rope_rr = [0]

        def rope(src, dst_v, t0, n):
            b = src.c0
            pr = rope_rr[0]
            rope_rr[0] ^= 1
            ia, ib = pr * 2, pr * 2 + 1
            ta, tb = TMP(ia, n), TMP(ib, n)
            cosv = V(COS_[:, 0, 0:n], "shr")
            sinv = V(SIN_[:, 0, 0:n], "shr")
            P.op("act", lambda e, o=tmp[0:64, ib, 0:n], a=ps[64:128, b, 0:n]: e.activation(out=o, in_=a, func=AF.Copy), reads=[src], writes=[tb])
            P.op("act", lambda e, o=tmp[64:128, ib, 0:n], a=ps[0:64, b, 0:n]: e.activation(out=o, in_=a, func=AF.Copy), reads=[src], writes=[tb])
            P.op("dve", lambda e, o=ta.ap, a=src.ap, c=cosv.ap: e.tensor_tensor(out=o, in0=a, in1=c, op=ALU.mult), reads=[src, cosv], writes=[ta])
            P.op("pool", lambda e, o=tb.ap, c=sinv.ap: e.tensor_tensor(out=o, in0=o, in1=c, op=ALU.mult), reads=[tb, sinv], writes=[tb])
            P.op("dve", lambda e, o=dst_v.ap, a=ta.ap, c=tb.ap: e.tensor_tensor(out=o, in0=a, in1=c, op=ALU.add), reads=[ta, tb], writes=[dst_v])

        cs_ctr = [0]

        def load_cs(u, t0, n):
            P.op("sp", lambda e, o=COS_[:, 0, 0:n], i=csd[u, 0, :, t0:t0 + n]: e.dma_start(out=o, in_=i), writes=[V(COS_[:, 0, 0:n], "shr")], dma_sem=cs_sem, group=("cs", u, t0, len(P.ops) // 2 * 0 + cs_ctr[0]))
            P.op("sp", lambda e, o=SIN_[:, 0, 0:n], i=csd[u, 1, :, t0:t0 + n]: e.dma_start(out=o, in_=i), writes=[V(SIN_[:, 0, 0:n], "shr")], dma_sem=cs_sem, group=("cs", u, t0, cs_ctr[0]))
            cs_ctr[0] += 1

        def attn_mixer(layer, u):
            j = layer // 2
            wqkv, wo = w_qkv_d[j], w_o_d[j]
            kvt, qt = ATT_KV[layer], ATT_Q[layer]
            nkb = sum(n for _, n in kvt) // 128
            for ki_, (t0, n) in enumerate(kvt):
                norm_h(0, layer, t0, n)
                load_cs(u, t0, n)
                hr = [Hv(k, n) for k in range(16)]
                rhs = [H_[:, k, 0:n] for k in range(16)]
                for kvp in range(2):
                    wk = load_slot(wview(wqkv, 2048 + kvp * 256, 256), 16, 256)
                    for jj in range(2):
                        kvh = kvp * 2 + jj
                        out = mm_group(nbank(), n, [wk[:, k, jj * 128:(jj + 1) * 128] for k in range(16)], rhs, [V(wk, "ring")] + hr)
                        rope(out, V(KT_[:, kvh, t0:t0 + n], "shr"), t0, n)
                if ki_ + 1 < len(kvt):
                    norm_stats(0, layer, kvt[ki_ + 1][0], kvt[ki_ + 1][1])
                else:
                    norm_stats(0, layer, qt[0][0], qt[0][1])
                for vp in range(2):
                    wv = load_slot(wview(wqkv, 2560 + vp * 256, 256), 16, 256)
                    for bi in range(n // 128):
                        blk = t0 // 128 + bi
                        out = mm_group(nbank(), 256, [H_[:, k, bi * 128:(bi + 1) * 128] for k in range(16)],
                                       [wv[:, k, :] for k in range(16)], [V(wv, "ring")] + [Hv(k, 128, bi * 128) for k in range(16)])
                        vv = V(VV_[:, blk, vp * 256:(vp + 1) * 256], "shr")
                        P.op("act", lambda e, o=vv.ap, i=out.ap: e.activation(out=o, in_=i, func=AF.Copy), reads=[out], writes=[vv])
            for qi, (t0, n) in enumerate(qt):
                norm_h(0, layer, t0, n)
                load_cs(u, t0, n)
                hr = [Hv(k, n) for k in range(16)]
                rhs = [H_[:, k, 0:n] for k in range(16)]
                for hp in range(8):
                    wq = load_slot(wview(wqkv, hp * 256, 256), 16, 256)
                    for jj in range(2):
                        hd = hp * 2 + jj
                        out = mm_group(nbank(), n, [wq[:, k, jj * 128:(jj + 1) * 128] for k in range(16)], rhs, [V(wq, "ring")] + hr)
                        rope(out, V(Q_[:, hd, 0:n], "shr"), t0, n)
                nbanks[0] = 8
                def scores(bi, g):
                    i = t0 // 128 + bi
                    nq = min(128, n - bi * 128)
                    kbs = [kb for kb in (i - 1, i, i + 1) if 0 <= kb < nkb]
                    qv = V(Q_[:, 4 * g:4 * g + 4, bi * 128:bi * 128 + nq], "shr")
                    pts = []
                    for ki, kb in enumerate(kbs):
                        ktv = V(KT_[:, g, kb * 128:(kb + 1) * 128], "shr")
                        so = PS(nbank(), 4 * nq)
                        P.op("pe", lambda e, o=so.ap, l=ktv.ap, r=qv.ap: e.matmul(o, l, r, start=True, stop=True), reads=[ktv, qv], writes=[so])
                        pi = (g % 2) * 3 + ki
                        pv = V(PT_[:, pi, 0:4 * nq], "shr")
                        P.op("act", lambda e, o=pv.ap, s_=so.ap: e.activation(out=o, in_=s_, func=AF.Exp, scale=SCALE), reads=[so], writes=[pv])
                        if kb != i:
                            mk = mask1 if kb == i - 1 else mask2
                            P.op("pool", lambda e, o=PT_[:, pi, 0:4 * nq].rearrange("p (a b) -> p a b", a=4), m_=mk, nq=nq:
                                 e.tensor_tensor(out=o, in0=o, in1=m_[:, 0:nq].unsqueeze(1).to_broadcast([128, 4, nq]), op=ALU.mult),
                                 reads=[pv, C_M1, C_M2], writes=[pv])
                        pts.append(pv)
                    return kbs, pts

                def finish(bi, g, kbs, pts):
                    nq = min(128, n - bi * 128)
                    db, ob_ = nbank(), nbank()
                    dn, oo = PS(db, 4 * nq), PS(ob_, 4 * nq)
                    L = len(pts)
                    for ki, pv in enumerate(pts):
                        P.op("pe", lambda e, o=dn.ap, r=pv.ap, st=(ki == 0), sp=(ki == L - 1): e.matmul(o, ones_bf[:, :], r, start=st, stop=sp),
                             reads=[pv, C_ONES], writes=[dn])
                    for ki, (kb, pv) in enumerate(zip(kbs, pts)):
                        vv = V(VV_[:, kb, g * 128:(g + 1) * 128], "shr")
                        P.op("pe", lambda e, o=oo.ap, l=vv.ap, r=pv.ap, st=(ki == 0), sp=(ki == L - 1): e.matmul(o, l, r, start=st, stop=sp),
                             reads=[vv, pv], writes=[oo])
                    ti_ = 4 + g % 2
                    rc = TMP(ti_, 4 * nq)
                    r3 = lambda ap: ap.rearrange("p (a b) -> p a b", a=4)
                    es_ap = esink[:, j * 16 + 4 * g:j * 16 + 4 * g + 4].unsqueeze(2).to_broadcast([128, 4, nq])
                    P.op("dve", lambda e, o=r3(tmp[:, ti_, 0:4 * nq]), a=r3(ps[:, db, 0:4 * nq]), s_=es_ap:
                         e.tensor_tensor(out=o, in0=a, in1=s_, op=ALU.add), reads=[dn, C_ES], writes=[rc])
                    if g != 3:
                        P.op("act", lambda e, o=rc.ap: e.activation(out=o, in_=o, func=AF.Ln), reads=[rc], writes=[rc])
                        P.op("act", lambda e, o=rc.ap: e.activation(out=o, in_=o, func=AF.Exp, scale=-1.0), reads=[rc], writes=[rc])
                    else:
                        P.op("dve", lambda e, o=rc.ap: e.reciprocal(out=o, in_=o), reads=[rc], writes=[rc])
                    ov = V(O_[:, 4 * g:4 * g + 4, bi * 128:bi * 128 + nq], "shr")
                    P.op("dve", lambda e, o=ov.ap, a=r3(ps[:, ob_, 0:4 * nq]), r=r3(tmp[:, ti_, 0:4 * nq]):
                         e.tensor_tensor(out=o, in0=a, in1=r, op=ALU.mult), reads=[oo, rc], writes=[ov])

                work = [(bi, g) for bi in range((n + 127) // 128) for g in range(4)]
                nxt = scores(*work[0])
                for wi, (bi, g) in enumerate(work):
                    cur = nxt
                    if wi + 1 < len(work):
                        nxt = scores(*work[wi + 1])
                    finish(bi, g, *cur)
                nbanks[0] = 6
                bank_s = out_proj(wo, O_, n, YS_low)
                if qi + 1 < len(qt):
                    nxt = (0, layer, qt[qi + 1][0], qt[qi + 1][1])
                else:
                    nxt = (2, layer, FFN_T[layer][0][0], FFN_T[layer][0][1])
                post_norm_update(1, layer, t0, n, YS_low, bank_s, nxt, True)

        P.op("sp", lambda e: e.dma_start(out=gv[:, :], in_=gvd[:, :]), writes=[V(gv[:, :], "gv")], dma_sem=const_sem, group="const")
        P.op("sp", lambda e: e.dma_start(out=sink[:, :], in_=skd[:, :]), writes=[V(sink[:, :], "sink")], dma_sem=const_sem, group="const")
        P.op("act", lambda e: e.activation(out=esink[:, :], in_=sink[:, :], func=AF.Exp), reads=[V(sink[:, :], "sink")], writes=[V(esink[:, :], "esink")])
        t_ones, t_m = TMP(0, 128), TMP(1, 128)
        P.op("pool", lambda e: e.memset(tmp[:, 0, 0:128], 1.0), writes=[t_ones])
        P.op("dve", lambda e: e.tensor_copy(out=ones_bf[:, :], in_=tmp[:, 0, 0:128]), reads=[t_ones], writes=[V(ones_bf[:, :], "ones")])
        P.op("pool", lambda e: e.affine_select(out=tmp[:, 1, 0:128], in_=tmp[:, 0, 0:128], pattern=[[-1, 128]], compare_op=ALU.is_ge, fill=0.0, base=0, channel_multiplier=1),
             reads=[t_ones], writes=[t_m])
        P.op("dve", lambda e: e.tensor_copy(out=mask1[:, :], in_=tmp[:, 1, 0:128]), reads=[t_m], writes=[V(mask1[:, :], "mask1")])
        P.op("pool", lambda e: e.affine_select(out=tmp[:, 1, 0:128], in_=tmp[:, 0, 0:128], pattern=[[1, 128]], compare_op=ALU.is_ge, fill=0.0, base=0, channel_multiplier=-1),
             reads=[t_ones], writes=[t_m])
        P.op("dve", lambda e: e.tensor_copy(out=mask2[:, :], in_=tmp[:, 1, 0:128]), reads=[t_m], writes=[V(mask2[:, :], "mask2")])
        for u in range(NUNITS):
            for c0 in range(0, NCH, 4):
                P.op("sp", lambda e, o=xbuf[:, c0:c0 + 4, :], i=xd[u].rearrange("(c p) t -> p c t", p=128)[:, c0:c0 + 4, :]: e.dma_start(out=o, in_=i),
                     writes=[V(xbuf[:, c0:c0 + 4, :], "xbuf")], dma_sem=xload_sem[c0 // 4])
            P.op("sp", lambda e, i=cwd[u]: e.dma_start(out=cw[:, :], in_=i), writes=[V(cw[:, :], "cw")], dma_sem=cw_sem)
            for layer in range(4):
                if layer % 2 == 0:
                    conv_mixer(layer)
                else:
                    attn_mixer(layer, u)
                ffn(layer)
            for c0 in range(0, NCH, 4):
                P.op("sp", lambda e, i=xbuf[:, c0:c0 + 4, 0:NREAL], o=yd[u].rearrange("(c p) t -> p c t", p=128)[:, c0:c0 + 4, :]: e.dma_start(out=o, in_=i),
                     reads=[V(xbuf[:, c0:c0 + 4, 0:NREAL], "xbuf")], dma_sem=ystore_sem[c0 // 4])

        run = P.emit(nc, es)
        with nc.Block() as block:
            @block.tensor
            def _(t):
                run("pe", t)

            @block.scalar
            def _(a):
                run("act", a)

            @block.vector
            def _(v):
                run("dve", v)

            @block.gpsimd
            def _(g):
                run("pool", g)

            @block.sync
            def _(s):
                run("sp", s, finals=ystore_sem)
    return nc


_NC_CACHE = {}


def _unit_list():
    units = []
    for s in range(20):
        units.append((s, 0))
        units.append((s, 1))
    return units


def kernel(x_prompt, x_sample, conv_w_in, conv_w_dw, conv_w_out, attn_w_qkv, attn_w_o, attn_sink,
           ffn_w_gate, ffn_w_up, ffn_w_down, g_mix_pre, g_mix_post, g_ffn_pre, g_ffn_post):
    f32 = np.float32
    xs = np.concatenate([np.asarray(x_prompt, f32), np.asarray(x_sample, f32)], axis=0)
    units = _unit_list()
    inv_freq = (1.0 / (10000.0 ** (np.arange(0, 128, 2, dtype=f32) / f32(128)))).astype(f32)
    pos_l = np.arange(NT, dtype=f32)
    pos_r = (SEQ - 1 - np.arange(NT)).astype(f32)
    tabs = []
    for pos in (pos_l, pos_r):
        ang = (pos[:, None] * inv_freq[None, :]).astype(f32)
        c = np.cos(ang).astype(f32).T
        s = np.sin(ang).astype(f32).T
        tabs.append(np.stack([np.concatenate([c, c], 0), np.concatenate([-s, s], 0)], 0))
    cwt = []
    dw = np.asarray(conv_w_dw, f32)
    for rev in (0, 1):
        a = dw[:, ::-1, :] if rev else dw
        cwt.append(np.ascontiguousarray(a.reshape(2, 3, NCH, 128).transpose(3, 0, 1, 2).reshape(128, 96)))
    g_all = np.stack([np.asarray(g, f32) for g in (g_mix_pre, g_mix_post, g_ffn_pre, g_ffn_post)], 0)
    gv = np.ascontiguousarray(g_all.reshape(4, 4, NCH, 128).transpose(3, 0, 1, 2).reshape(128, 256))
    sink = np.ascontiguousarray(np.broadcast_to(np.asarray(attn_sink, f32).reshape(1, 32), (128, 32)))
    shared = {
        "gv_tab": gv, "sink_tab": sink,
        "conv_w_in": np.asarray(conv_w_in, f32), "conv_w_out": np.asarray(conv_w_out, f32),
        "attn_w_qkv": np.asarray(attn_w_qkv, f32), "attn_w_o": np.asarray(attn_w_o, f32),
        "ffn_w_gate": np.asarray(ffn_w_gate, f32), "ffn_w_up": np.asarray(ffn_w_up, f32),
        "ffn_w_down": np.asarray(ffn_w_down, f32),
    }
    in_maps = []
    for c in range(NCORES):
        xu = np.empty((NUNITS, D, NT), f32)
        cs = np.empty((NUNITS, 2, 128, NT), f32)
        cwu = np.empty((NUNITS, 128, 96), f32)
        for k in range(NUNITS):
            s, rev = units[c * NUNITS + k]
            seq = xs[s]
            if rev:
                seq = seq[::-1]
            xu[k] = seq[:NT].T
            cs[k] = tabs[rev]
            cwu[k] = cwt[rev]
        m = dict(shared)
        m.update({"x_units": xu, "cs_tab": cs, "cw_tab": cwu})
        in_maps.append(m)
    if "nc" not in _NC_CACHE:
        _NC_CACHE["nc"] = build_nc()
    res = run_bass_kernel_spmd(_NC_CACHE["nc"], in_maps, core_ids=list(range(NCORES)))
    out = np.empty((20, SEQ, D), f32)
    for c in range(NCORES):
        yu = res.results[c]["y_units"]
        for k in range(NUNITS):
            s, rev = units[c * NUNITS + k]
            blk = yu[k].T
            if rev:
                out[s, SEQ - NREAL:] = blk[::-1]
            else:
                out[s, :NREAL] = blk
    return out[:16].copy(), out[16:].copy()
```

```python
import math
from contextlib import ExitStack

import numpy as np
import concourse.bass as bass
import concourse.mybir as mybir
from concourse.bass_utils import run_bass_kernel_spmd

F32 = mybir.dt.float32
BF16 = mybir.dt.bfloat16
ALU = mybir.AluOpType
AF = mybir.ActivationFunctionType

D = 2048
NCH = 16
DFF = 5632
NF = 44
NT = 1408
NREAL = 1024
NUNITS = 5
NCORES = 8
SEQ = 2048
EPS = 1e-6
NH = 16
NKV = 4
QKV = 3072
SCALE = 128 ** -0.5
CELL = 512
NSLOT = 3
SLOT_E = 4096
NSCR = 408

CONV_U = {0: [(0, 428), (428, 427), (855, 427)], 2: [(0, 385), (385, 384), (769, 384)]}
CONV_Z = {0: [(0, 428), (428, 427), (855, 426)], 2: [(0, 385), (385, 384), (769, 383)]}
ATT_KV = {1: [(0, 512), (512, 512), (1024, 384)], 3: [(0, 384), (384, 384), (768, 384)]}
ATT_Q = {1: [(0, 512), (512, 384), (896, 257)], 3: [(0, 512), (512, 512)]}
FFN_T = {0: [(0, 427), (427, 427), (854, 427)], 1: [(0, 385), (385, 384), (769, 384)],
         2: [(0, 384), (384, 384), (768, 384)], 3: [(0, 512), (512, 512)]}


class V:
    __slots__ = ("ap", "tid", "c0", "c1")

    def __init__(self, ap, tid, c0=None, c1=None):
        self.ap = ap
        self.tid = tid
        if c0 is None:
            pairs = ap.ap
            row = pairs[0][0]
            off = int(ap.offset) % row if row > 0 else int(ap.offset)
            ext = sum((cnt - 1) * st for st, cnt in pairs[1:])
            es = 4 if ap.dtype == F32 else 2
            c0 = (off * es) // CELL
            c1 = ((off + ext + 1) * es + CELL - 1) // CELL
        self.c0 = c0
        self.c1 = c1


class Prog:
    ENGS = ("pe", "act", "dve", "pool", "sp")

    def __init__(self):
        self.ops = []
        self.lastw = {}
        self.readers = {}

    def op(self, eng, fn, reads=(), writes=(), dma_sem=None, group=None):
        idx = len(self.ops)
        deps = set()
        lastw, readers = self.lastw, self.readers
        for v in reads:
            for c in range(v.c0, v.c1):
                w = lastw.get((v.tid, c))
                if w is not None:
                    deps.add(w)
                if v.tid == "ps":
                    r = readers.get((v.tid, c))
                    if r:
                        deps.update(i_ for k_, i_ in r.items() if k_ != eng)
        for v in writes:
            for c in range(v.c0, v.c1):
                k = (v.tid, c)
                w = lastw.get(k)
                if w is not None:
                    deps.add(w)
                r = readers.get(k)
                if r:
                    deps.update(r.values())
        rkey = eng if dma_sem is None else ("dma", idx)
        for v in writes:
            for c in range(v.c0, v.c1):
                k = (v.tid, c)
                lastw[k] = idx
                readers[k] = {}
        for v in reads:
            for c in range(v.c0, v.c1):
                k = (v.tid, c)
                r = readers.get(k)
                if r is None:
                    r = readers[k] = {}
                r[rkey] = idx
        deps.discard(idx)
        self.ops.append([eng, fn, deps, dma_sem, False, group if group is not None else idx])
        return idx

    def emit(self, nc, es):
        ops = self.ops
        eng_sem = {e: es.enter_context(nc.semaphore("c_" + e)) for e in self.ENGS}
        def need(op, d):
            dop = ops[d]
            if dop[3] is not None:
                return True
            if dop[0] == op[0] and op[0] == "pe" and op[3] is None:
                return False
            return True
        for op in ops:
            for d in op[2]:
                if need(op, d):
                    ops[d][4] = True
        cnt = {e: 0 for e in self.ENGS}
        dcnt = {}
        token = [None] * len(ops)
        gend = {}
        for i, op in enumerate(ops):
            if op[3] is not None:
                s = op[3]
                dcnt[s] = dcnt.get(s, 0) + 16
                gend[(id(s), op[5])] = dcnt[s]
        for i, op in enumerate(ops):
            if op[3] is not None:
                s = op[3]
                token[i] = (s, gend[(id(s), op[5])])
            elif op[4]:
                cnt[op[0]] += 1
                token[i] = (eng_sem[op[0]], cnt[op[0]])
        per_eng = {e: [] for e in self.ENGS}
        for i, op in enumerate(ops):
            per_eng[op[0]].append(i)
        final_dma = dict(dcnt)

        def run(ename, eh, finals=()):
            waited = {}
            for i in per_eng[ename]:
                op = ops[i]
                w = {}
                for d in op[2]:
                    if need(op, d):
                        s, val = token[d]
                        if val > w.get(s, 0):
                            w[s] = val
                for s, val in w.items():
                    if val > waited.get(s, 0):
                        eh.wait_ge(s, val)
                        waited[s] = val
                ins = op[1](eh)
                if op[3] is not None:
                    ins.then_inc(op[3], 16)
                elif op[4]:
                    ins.then_inc(eng_sem[ename], 1)
            for s in finals:
                if s in final_dma:
                    eh.wait_ge(s, final_dma[s])
        return run


def build_nc():
    nc = bass.Bass("TRN2", target_bir_lowering=False)
    dt_ = nc.dram_tensor
    xd = dt_("x_units", [NUNITS, D, NT], F32, kind="ExternalInput").ap()
    csd = dt_("cs_tab", [NUNITS, 2, 128, NT], F32, kind="ExternalInput").ap()
    cwd = dt_("cw_tab", [NUNITS, 128, 96], F32, kind="ExternalInput").ap()
    gvd = dt_("gv_tab", [128, 256], F32, kind="ExternalInput").ap()
    skd = dt_("sink_tab", [128, 32], F32, kind="ExternalInput").ap()
    w_in_d = dt_("conv_w_in", [2, D, 3 * D], F32, kind="ExternalInput").ap()
    w_out_d = dt_("conv_w_out", [2, D, D], F32, kind="ExternalInput").ap()
    w_qkv_d = dt_("attn_w_qkv", [2, D, QKV], F32, kind="ExternalInput").ap()
    w_o_d = dt_("attn_w_o", [2, D, D], F32, kind="ExternalInput").ap()
    w_g_d = dt_("ffn_w_gate", [4, D, DFF], F32, kind="ExternalInput").ap()
    w_u_d = dt_("ffn_w_up", [4, D, DFF], F32, kind="ExternalInput").ap()
    w_d_d = dt_("ffn_w_down", [4, DFF, D], F32, kind="ExternalInput").ap()
    yd = dt_("y_units", [NUNITS, D, NREAL], F32, kind="ExternalOutput").ap()
    scr_parts = [dt_("wscr%d" % i, [102, 128, SLOT_E], BF16, kind="Internal").ap() for i in range(4)]

    es = ExitStack()
    with es:
        sb = lambda name, shape, dt: es.enter_context(nc.sbuf_tensor(name, shape, dt))
        xbuf = sb("xbuf", [128, NCH, NT], F32)
        shr = sb("shr", [128, 35840], BF16)
        ring = sb("ring", [128, NSLOT, SLOT_E], BF16)
        gv = sb("gv", [128, 256], F32)
        cw = sb("cw", [128, 96], F32)
        sink = sb("sink", [128, 32], F32)
        esink = sb("esink", [128, 32], F32)
        ones_bf = sb("ones_bf", [128, 128], BF16)
        mask1 = sb("mask1", [128, 128], BF16)
        mask2 = sb("mask2", [128, 128], BF16)
        rt = sb("rt", [128, 2, 512], F32)
        sq = sb("sq", [128, 4, 512], BF16)
        tmp = sb("tmp", [128, 6, 600], F32)
        small = sb("small", [128, 8, 16], F32)
        ps = es.enter_context(nc.psum_tensor("ps", [128, 8, 512], F32))
        sem = lambda n: es.enter_context(nc.semaphore(n))
        slot_sem = [sem("slot%d" % i) for i in range(NSLOT)]
        scrst_sem = [sem("scrst%d" % i) for i in range(NSLOT)]
        scr_idx = {}
        cs_sem, cw_sem, const_sem = sem("cs"), sem("cwl"), sem("cst")
        xload_sem = [sem("xl%d" % i) for i in range(4)]
        ystore_sem = [sem("ys%d" % i) for i in range(4)]

        P = Prog()

        def X(c, t0, n):
            return V(xbuf[:, c, t0:t0 + n], "xbuf")

        def shr_bf(e0, a, b):
            return shr[:, e0:e0 + a * b].rearrange("p (a b) -> p a b", a=a)

        def shr_f32(e0, a, b):
            return shr[:, e0:e0 + 2 * a * b].bitcast(F32).rearrange("p (a b) -> p a b", a=a)

        K = 512
        H_ = shr_bf(0, 16, 512)
        def Hv(c, n, t0=0):
            return V(H_[:, c, t0:t0 + n], "shr")
        ACT_ = shr_bf(16 * K, 22, 512)
        YS_ffn = shr_f32(38 * K, 16, 512)
        YS_low = shr_f32(0, 16, 512)
        ZC = 432
        YS_conv = shr_f32(16 * K, 16, ZC)
        Z_ = [shr_bf(16 * K + 16 * ZC * 2, 16, ZC), shr_bf(16 * K + 16 * ZC * 3, 16, ZC)]
        Q_ = shr_bf(16 * K, 16, 512)
        O_ = shr_bf(32 * K, 16, 512)
        COS_ = shr_f32(32 * K, 1, 512)
        SIN_ = shr_f32(34 * K, 1, 512)
        PT_ = shr_bf(0, 6, 512)
        KT_ = shr[:, 48 * K:48 * K + 4 * NT].rearrange("p (a b) -> p a b", a=4)
        VV_ = shr_bf(48 * K + 4 * NT, 11, 512)

        def PS(b, n=512, p0=0, p1=128):
            return V(ps[p0:p1, b, 0:n], "ps", b, b + 1)

        def TMP(i, n, o=0):
            return V(tmp[:, i, o:o + n], "tmp")

        bank_rr = [0]
        nbanks = [6]
        def nbank():
            b = bank_rr[0] % nbanks[0]
            bank_rr[0] = (b + 1) % nbanks[0]
            return b
        stat_rr = [0]
        def sbank():
            b = 6 + stat_rr[0]
            stat_rr[0] ^= 1
            return b
        slot_rr = [0]
        rt_rr = [0]
        sq_rr = [0]

        def GV(kind, layer, c):
            i = (kind * 4 + layer) * 16 + c
            return gv[:, i:i + 1]

        C_ONES = V(ones_bf[:, :], "ones")
        C_M1 = V(mask1[:, :], "mask1")
        C_M2 = V(mask2[:, :], "mask2")
        C_GV = V(gv[:, :], "gv")
        C_ES = V(esink[:, :], "esink")
        C_CW = V(cw[:, :], "cw")

        def load_slot(src_ap, k, ncols):
            s = slot_rr[0]
            slot_rr[0] = (s + 1) % NSLOT
            ne = k * ncols
            view = ring[:, s, 0:ne].rearrange("p (k c) -> p k c", k=k)
            vv = V(view, "ring")
            key = (src_ap.tensor.name, int(src_ap.offset), tuple(src_ap.ap))
            if key not in scr_idx:
                si = scr_idx[key] = len(scr_idx)
                assert si < NSCR
                P.op("pool", lambda e, o=view, i=src_ap: e.dma_start(out=o, in_=i), writes=[vv], dma_sem=slot_sem[s])
                P.op("sp", lambda e, o=scr_parts[si // 102][si % 102, :, 0:ne], i=ring[:, s, 0:ne]: e.dma_start(out=o, in_=i),
                     reads=[vv], writes=[V(None, "scr", si, si + 1)], dma_sem=scrst_sem[s])
            else:
                si = scr_idx[key]
                P.op("sp", lambda e, o=ring[:, s, 0:ne], i=scr_parts[si // 102][si % 102, :, 0:ne]: e.dma_start(out=o, in_=i),
                     reads=[V(None, "scr", si, si + 1)], writes=[vv], dma_sem=slot_sem[s])
            return view

        def wview(w2d, c0, ncols, r0=0, k=16):
            return w2d.rearrange("(k p) c -> p k c", p=128)[:, r0:r0 + k, c0:c0 + ncols]

        def mm_group(bank, n, lhs_list, rhs_list, reads, p1=128):
            out = PS(bank, n, 0, p1)
            L = len(lhs_list)
            for i in range(L):
                P.op("pe", lambda e, o=out.ap, l=lhs_list[i], r=rhs_list[i], st=(i == 0), sp=(i == L - 1):
                     e.matmul(o, l, r, start=st, stop=sp), reads=reads if i == 0 else reads, writes=[out])
            return out

        def stats_begin():
            return sbank()

        def stats_sq(n, src_v):
            s = sq_rr[0]
            sq_rr[0] = (s + 1) % 4
            sv = V(sq[:, s, 0:n], "sq")
            P.op("act", lambda e, o=sv.ap, i=src_v.ap: e.activation(out=o, in_=i, func=AF.Square), reads=[src_v], writes=[sv])
            return sv

        def stats_mm(bank, n, sv, first, last):
            out = PS(bank, n)
            P.op("pe", lambda e, o=out.ap, r=sv.ap, st=first, sp=last: e.matmul(o, ones_bf[:, :], r, start=st, stop=sp),
                 reads=[sv, C_ONES], writes=[out])

        def stats_add(bank, n, src_v, first, last):
            stats_mm(bank, n, stats_sq(n, src_v), first, last)

        def stats_finish(bank, n):
            i = rt_rr[0]
            rt_rr[0] ^= 1
            rv = V(rt[:, i, 0:n], "rt")
            src = PS(bank, n)
            P.op("act", lambda e, o=rv.ap, s_=src.ap: e.activation(out=o, in_=s_, func=AF.Sqrt, scale=1.0 / D, bias=EPS),
                 reads=[src], writes=[rv])
            P.op("dve", lambda e, o=rv.ap: e.reciprocal(out=o, in_=o), reads=[rv], writes=[rv])
            return rv

        pre_r = {}

        def norm_stats(kind, layer, t0, n):
            bank = stats_begin()
            for c in range(NCH):
                stats_add(bank, n, X(c, t0, n), c == 0, c == NCH - 1)
            pre_r[(kind, layer, t0, n)] = stats_finish(bank, n)

        pre_h = set()

        def norm_h(kind, layer, t0, n):
            if (kind, layer, t0, n) in pre_h:
                pre_h.discard((kind, layer, t0, n))
                return
            if (kind, layer, t0, n) not in pre_r:
                norm_stats(kind, layer, t0, n)
            rv = pre_r.pop((kind, layer, t0, n))
            for c in range(NCH):
                hv = Hv(c, n)
                xv = X(c, t0, n)
                P.op("dve", lambda e, o=hv.ap, a=xv.ap, g=GV(kind, layer, c), r=rv.ap:
                     e.scalar_tensor_tensor(out=o, in0=a, scalar=g, in1=r, op0=ALU.mult, op1=ALU.mult),
                     reads=[xv, rv, C_GV], writes=[hv])

        def post_norm_update(kind, layer, t0, n, ys3, bank, nxt=None, h_next=False):
            if nxt is not None and nxt not in pre_r:
                norm_stats(*nxt)
            rv = stats_finish(bank, n)

            def upd(m):
                yv = V(ys3[:, m, 0:n], "shr")
                xv = X(m, t0, n)
                P.op("dve", lambda e, o=yv.ap, r=rv.ap: e.tensor_tensor(out=o, in0=o, in1=r, op=ALU.mult),
                     reads=[yv, rv], writes=[yv])
                P.op("dve", lambda e, o=xv.ap, y=yv.ap, g=GV(kind, layer, m):
                     e.scalar_tensor_tensor(out=o, in0=y, scalar=g, in1=o, op0=ALU.mult, op1=ALU.add),
                     reads=[yv, xv, C_GV], writes=[xv])
            for m in range(8):
                upd(m)
            if nxt is not None and h_next:
                norm_h(*nxt)
                pre_h.add(nxt)
            for m in range(8, NCH):
                upd(m)

        def out_proj(w2d, rhs3, n, ys3, kchunks=16):
            bank_s = stats_begin()
            pend = []
            for mp in range(8):
                wv = load_slot(wview(w2d, mp * 256, 256), 16, 256)
                for jj in range(2):
                    m = mp * 2 + jj
                    b = nbank()
                    lhs = [wv[:, k, jj * 128:(jj + 1) * 128] for k in range(16)]
                    rhs = [rhs3[:, k, 0:n] for k in range(16)]
                    rd = [V(wv, "ring")] + [V(rhs3[:, k, 0:n], "shr") for k in range(16)]
                    out = mm_group(b, n, lhs, rhs, rd)
                    yv = V(ys3[:, m, 0:n], "shr")
                    P.op("act", lambda e, o=yv.ap, i=out.ap: e.activation(out=o, in_=i, func=AF.Copy), reads=[out], writes=[yv])
                    pend.append((stats_sq(n, yv), m))
                    if len(pend) > 2:
                        sv0, m0 = pend.pop(0)
                        stats_mm(bank_s, n, sv0, m0 == 0, m0 == 15)
            for sv0, m0 in pend:
                stats_mm(bank_s, n, sv0, m0 == 0, m0 == 15)
            return bank_s

        def ffn(layer):
            wg, wu, wd = w_g_d[layer], w_u_d[layer], w_d_d[layer]
            tiles = FFN_T[layer]
            norm_h(2, layer, tiles[0][0], tiles[0][1])
            for ti, (t0, n) in enumerate(tiles):
                bank_s = None
                for half in range(2):
                    for fp in range(11):
                        f0 = half * 22 + fp * 2
                        gvw = load_slot(wview(wg, f0 * 128, 256), 16, 256)
                        uvw = load_slot(wview(wu, f0 * 128, 256), 16, 256)
                        for jj in range(2):
                            fl = fp * 2 + jj
                            rhs = [H_[:, k, 0:n] for k in range(16)]
                            hr = [Hv(k, n) for k in range(16)]
                            bg = nbank()
                            og = mm_group(bg, n, [gvw[:, k, jj * 128:(jj + 1) * 128] for k in range(16)], rhs, [V(gvw, "ring")] + hr)
                            bu = nbank()
                            ou = mm_group(bu, n, [uvw[:, k, jj * 128:(jj + 1) * 128] for k in range(16)], rhs, [V(uvw, "ring")] + hr)
                            tv = TMP(fl % 2, n)
                            P.op("act", lambda e, o=tv.ap, i=og.ap: e.activation(out=o, in_=i, func=AF.Silu), reads=[og], writes=[tv])
                            av = V(ACT_[:, fl, 0:n], "shr")
                            P.op("dve", lambda e, o=av.ap, a=ou.ap, b_=tv.ap: e.tensor_tensor(out=o, in0=a, in1=b_, op=ALU.mult),
                                 reads=[ou, tv], writes=[av])
                    if half == 1:
                        bank_s = stats_begin()
                        pend = []
                    for mp in range(8):
                        if half == 1 and mp == 0:
                            if ti + 1 < len(tiles):
                                norm_h(2, layer, tiles[ti + 1][0], tiles[ti + 1][1])
                            elif layer < 3:
                                ft = (CONV_U if (layer + 1) % 2 == 0 else ATT_KV)[layer + 1][0]
                                key = (0, layer + 1, ft[0], ft[1])
                                norm_h(*key)
                                pre_h.add(key)
                        banks = [nbank(), nbank()]
                        outs = [None, None]
                        for kq in range(2):
                            wv = load_slot(wd.rearrange("(f p) c -> p f c", p=128)[:, half * 22 + kq * 11: half * 22 + kq * 11 + 11, mp * 256:mp * 256 + 256], 11, 256)
                            for jj in range(2):
                                out = PS(banks[jj], n)
                                for kk in range(11):
                                    fl = kq * 11 + kk
                                    av = V(ACT_[:, fl, 0:n], "shr")
                                    P.op("pe", lambda e, o=out.ap, l=wv[:, kk, jj * 128:(jj + 1) * 128], r=av.ap, st=(kq == 0 and kk == 0), sp=(kq == 1 and kk == 10):
                                         e.matmul(o, l, r, start=st, stop=sp), reads=[V(wv, "ring"), av], writes=[out])
                                outs[jj] = out
                        for jj in range(2):
                            m = mp * 2 + jj
                            yv = V(YS_ffn[:, m, 0:n], "shr")
                            if half == 0:
                                P.op("act", lambda e, o=yv.ap, i=outs[jj].ap: e.activation(out=o, in_=i, func=AF.Copy), reads=[outs[jj]], writes=[yv])
                            else:
                                P.op("dve", lambda e, o=yv.ap, i=outs[jj].ap: e.tensor_tensor(out=o, in0=i, in1=o, op=ALU.add), reads=[outs[jj], yv], writes=[yv])
                                pend.append((stats_sq(n, yv), m))
                                if len(pend) > 2:
                                    sv0, m0 = pend.pop(0)
                                    stats_mm(bank_s, n, sv0, m0 == 0, m0 == 15)
                    if half == 1:
                        for sv0, m0 in pend:
                            stats_mm(bank_s, n, sv0, m0 == 0, m0 == 15)
                post_norm_update(3, layer, t0, n, YS_ffn, bank_s)

        def conv_mixer(layer):
            j = layer // 2
            win, wout = w_in_d[j], w_out_d[j]
            ut, zt = CONV_U[layer], CONV_Z[layer]
            nt_ = len(ut)
            def CW(t, c):
                i = (j * 3 + t) * 16 + c
                return cw[:, i:i + 1]
            w2arr = cw[:, (j * 3 + 2) * 16:(j * 3 + 2) * 16 + 16]
            def SM(i, c=None):
                if c is None:
                    return V(small[:, i, :], "small")
                return V(small[:, i, c:c + 1], "small")

            def U(ti):
                t0, nu = ut[ti]
                nz = zt[ti][1]
                last = (ti == nt_ - 1)
                par = ti % 2
                norm_h(0, layer, t0, nu)
                hr = [Hv(k, nu) for k in range(16)]
                rhs = [H_[:, k, 0:nu] for k in range(16)]
                for mp in range(8):
                    wb = load_slot(wview(win, mp * 256, 256), 16, 256)
                    wc = load_slot(wview(win, D + mp * 256, 256), 16, 256)
                    wh = load_slot(wview(win, 2 * D + mp * 256, 256), 16, 256)
                    for jj in range(2):
                        m = mp * 2 + jj
                        sl = slice(jj * 128, (jj + 1) * 128)
                        ob = mm_group(nbank(), nu, [wb[:, k, sl] for k in range(16)], rhs, [V(wb, "ring")] + hr)
                        oc = mm_group(nbank(), nu, [wc[:, k, sl] for k in range(16)], rhs, [V(wc, "ring")] + hr)
                        oh = mm_group(nbank(), nu, [wh[:, k, sl] for k in range(16)], rhs, [V(wh, "ring")] + hr)
                        pb = (m % 2) * 3
                        ub = TMP(pb, nu + 2)
                        t1 = TMP(pb + 1, nz)
                        h3 = TMP(pb + 2, nu)
                        P.op("act", lambda e, o=h3.ap, i=oh.ap: e.activation(out=o, in_=i, func=AF.Copy), reads=[oh], writes=[h3])
                        uin = V(tmp[:, pb, 1:nu + 1], "tmp")
                        P.op("dve", lambda e, o=uin.ap, a=oc.ap, b_=h3.ap: e.tensor_tensor(out=o, in0=a, in1=b_, op=ALU.mult),
                             reads=[oc, h3], writes=[uin])
                        u0 = V(tmp[:, pb, 0:1], "tmp")
                        if ti == 0:
                            P.op("dve", lambda e, o=u0.ap: e.memset(o, 0.0), writes=[u0])
                        else:
                            cv = SM(1 - par, m)
                            P.op("dve", lambda e, o=u0.ap, i=cv.ap: e.tensor_copy(out=o, in_=i), reads=[cv], writes=[u0])
                        if not last:
                            P.op("dve", lambda e, o=SM(par, m).ap, i=tmp[:, pb, nu:nu + 1]: e.tensor_copy(out=o, in_=i),
                                 reads=[uin], writes=[SM(par, m)])
                        if ti > 0:
                            P.op("dve", lambda e, o=SM(6, m).ap, i=tmp[:, pb, 1:2]: e.tensor_copy(out=o, in_=i),
                                 reads=[uin], writes=[SM(6, m)])
                        P.op("act", lambda e, o=t1.ap, i=tmp[:, pb, 1:nz + 1], s_=CW(1, m): e.activation(out=o, in_=i, func=AF.Copy, scale=s_),
                             reads=[uin, C_CW], writes=[t1])
                        P.op("dve", lambda e, o=t1.ap, i=tmp[:, pb, 0:nz], s_=CW(0, m):
                             e.scalar_tensor_tensor(out=o, in0=i, scalar=s_, in1=o, op0=ALU.mult, op1=ALU.add),
                             reads=[ub, t1, C_CW], writes=[t1])
                        nw2 = min(nz, nu - 1)
                        t1w = V(tmp[:, pb + 1, 0:nw2], "tmp")
                        P.op("dve", lambda e, o=t1w.ap, i=tmp[:, pb, 2:nw2 + 2], s_=CW(2, m):
                             e.scalar_tensor_tensor(out=o, in0=i, scalar=s_, in1=o, op0=ALU.mult, op1=ALU.add),
                             reads=[ub, t1w, C_CW], writes=[t1w])
                        zv = V(Z_[par][:, m, 0:nz], "shr")
                        P.op("dve", lambda e, o=zv.ap, a=ps[:, ob.c0, 0:nz], b_=t1.ap: e.tensor_tensor(out=o, in0=a, in1=b_, op=ALU.mult),
                             reads=[ob, t1], writes=[zv])
                        if not last:
                            P.op("dve", lambda e, o=SM(2 + par, m).ap, i=ps[:, ob.c0, nz - 1:nz]: e.tensor_copy(out=o, in_=i),
                                 reads=[ob], writes=[SM(2 + par, m)])
                            P.op("dve", lambda e, o=SM(4 + par, m).ap, i=tmp[:, pb + 1, nz - 1:nz]: e.tensor_copy(out=o, in_=i),
                                 reads=[t1], writes=[SM(4 + par, m)])

            def fix(ti):
                par = ti % 2
                nz = zt[ti][1]
                sc = SM(7)
                P.op("dve", lambda e, o=sc.ap, a=SM(6).ap: e.tensor_tensor(out=o, in0=a, in1=w2arr, op=ALU.mult), reads=[SM(6), C_CW], writes=[sc])
                P.op("dve", lambda e, o=sc.ap, a=SM(4 + par).ap: e.tensor_tensor(out=o, in0=o, in1=a, op=ALU.add), reads=[sc, SM(4 + par)], writes=[sc])
                zl = V(Z_[par][:, :, nz - 1], "shr")
                P.op("dve", lambda e, o=zl.ap, a=sc.ap, b_=SM(2 + par).ap: e.tensor_tensor(out=o, in0=a, in1=b_, op=ALU.mult),
                     reads=[sc, SM(2 + par)], writes=[zl])

            def Z(ti):
                t0, nz = zt[ti]
                if ti + 2 < nt_:
                    nxt = (0, layer, ut[ti + 2][0], ut[ti + 2][1])
                elif ti == nt_ - 1:
                    nxt = (2, layer, FFN_T[layer][0][0], FFN_T[layer][0][1])
                else:
                    nxt = None
                if nxt is not None:
                    norm_h(*nxt)
                    pre_h.add(nxt)
                bank_s = out_proj(wout, Z_[ti % 2], nz, YS_conv)
                post_norm_update(1, layer, t0, nz, YS_conv, bank_s)

            U(0)
            for ti in range(1, nt_):
                U(ti)
                fix(ti - 1)
                Z(ti - 1)
            Z(nt_ - 1)

        rope_rr = [0]

        def rope(src, dst_v, t0, n):
            b = src.c0
            pr = rope_rr[0]
            rope_rr[0] ^= 1
            ia, ib = pr * 2, pr * 2 + 1
            ta, tb = TMP(ia, n), TMP(ib, n)
            cosv = V(COS_[:, 0, 0:n], "shr")
            sinv = V(SIN_[:, 0, 0:n], "shr")
            P.op("act", lambda e, o=tmp[0:64, ib, 0:n], a=ps[64:128, b, 0:n]: e.activation(out=o, in_=a, func=AF.Copy), reads=[src], writes=[tb])
            P.op("act", lambda e, o=tmp[64:128, ib, 0:n], a=ps[0:64, b, 0:n]: e.activation(out=o, in_=a, func=AF.Copy), reads=[src], writes=[tb])
            P.op("dve", lambda e, o=ta.ap, a=src.ap, c=cosv.ap: e.tensor_tensor(out=o, in0=a, in1=c, op=ALU.mult), reads=[src, cosv], writes=[ta])
            P.op("pool", lambda e, o=tb.ap, c=sinv.ap: e.tensor_tensor(out=o, in0=o, in1=c, op=ALU.mult), reads=[tb, sinv], writes=[tb])
            P.op("dve", lambda e, o=dst_v.ap, a=ta.ap, c=tb.ap: e.tensor_tensor(out=o, in0=a, in1=c, op=ALU.add), reads=[ta, tb], writes=[dst_v])

        cs_ctr = [0]

        def load_cs(u, t0, n):
            P.op("sp", lambda e, o=COS_[:, 0, 0:n], i=csd[u, 0, :, t0:t0 + n]: e.dma_start(out=o, in_=i), writes=[V(COS_[:, 0, 0:n], "shr")], dma_sem=cs_sem, group=("cs", u, t0, len(P.ops) // 2 * 0 + cs_ctr[0]))
            P.op("sp", lambda e, o=SIN_[:, 0, 0:n], i=csd[u, 1, :, t0:t0 + n]: e.dma_start(out=o, in_=i), writes=[V(SIN_[:, 0, 0:n], "shr")], dma_sem=cs_sem, group=("cs", u, t0, cs_ctr[0]))
            cs_ctr[0] += 1

        def attn_mixer(layer, u):
            j = layer // 2
            wqkv, wo = w_qkv_d[j], w_o_d[j]
            kvt, qt = ATT_KV[layer], ATT_Q[layer]
            nkb = sum(n for _, n in kvt) // 128
            for ki_, (t0, n) in enumerate(kvt):
                norm_h(0, layer, t0, n)
                load_cs(u, t0, n)
                hr = [Hv(k, n) for k in range(16)]
                rhs = [H_[:, k, 0:n] for k in range(16)]
                for kvp in range(2):
                    wk = load_slot(wview(wqkv, 2048 + kvp * 256, 256), 16, 256)
                    for jj in range(2):
                        kvh = kvp * 2 + jj
                        out = mm_group(nbank(), n, [wk[:, k, jj * 128:(jj + 1) * 128] for k in range(16)], rhs, [V(wk, "ring")] + hr)
                        rope(out, V(KT_[:, kvh, t0:t0 + n], "shr"), t0, n)
                if ki_ + 1 < len(kvt):
                    norm_stats(0, layer, kvt[ki_ + 1][0], kvt[ki_ + 1][1])
                else:
                    norm_stats(0, layer, qt[0][0], qt[0][1])
                for vp in range(2):
                    wv = load_slot(wview(wqkv, 2560 + vp * 256, 256), 16, 256)
                    for bi in range(n // 128):
                        blk = t0 // 128 + bi
                        out = mm_group(nbank(), 256, [H_[:, k, bi * 128:(bi + 1) * 128] for k in range(16)],
                                       [wv[:, k, :] for k in range(16)], [V(wv, "ring")] + [Hv(k, 128, bi * 128) for k in range(16)])
                        vv = V(VV_[:, blk, vp * 256:(vp + 1) * 256], "shr")
                        P.op("act", lambda e, o=vv.ap, i=out.ap: e.activation(out=o, in_=i, func=AF.Copy), reads=[out], writes=[vv])
            for qi, (t0, n) in enumerate(qt):
                norm_h(0, layer, t0, n)
                load_cs(u, t0, n)
                hr = [Hv(k, n) for k in range(16)]
                rhs = [H_[:, k, 0:n] for k in range(16)]
                for hp in range(8):
                    wq = load_slot(wview(wqkv, hp * 256, 256), 16, 256)
                    for jj in range(2):
                        hd = hp * 2 + jj
                        out = mm_group(nbank(), n, [wq[:, k, jj * 128:(jj + 1) * 128] for k in range(16)], rhs, [V(wq, "ring")] + hr)
                        rope(out, V(Q_[:, hd, 0:n], "shr"), t0, n)
                nbanks[0] = 8
                def scores(bi, g):
                    i = t0 // 128 + bi
                    nq = min(128, n - bi * 128)
                    kbs = [kb for kb in (i - 1, i, i + 1) if 0 <= kb < nkb]
                    qv = V(Q_[:, 4 * g:4 * g + 4, bi * 128:bi * 128 + nq], "shr")
                    pts = []
                    for ki, kb in enumerate(kbs):
                        ktv = V(KT_[:, g, kb * 128:(kb + 1) * 128], "shr")
                        so = PS(nbank(), 4 * nq)
                        P.op("pe", lambda e, o=so.ap, l=ktv.ap, r=qv.ap: e.matmul(o, l, r, start=True, stop=True), reads=[ktv, qv], writes=[so])
                        pi = (g % 2) * 3 + ki
                        pv = V(PT_[:, pi, 0:4 * nq], "shr")
                        P.op("act", lambda e, o=pv.ap, s_=so.ap: e.activation(out=o, in_=s_, func=AF.Exp, scale=SCALE), reads=[so], writes=[pv])
                        if kb != i:
                            mk = mask1 if kb == i - 1 else mask2
                            P.op("pool", lambda e, o=PT_[:, pi, 0:4 * nq].rearrange("p (a b) -> p a b", a=4), m_=mk, nq=nq:
                                 e.tensor_tensor(out=o, in0=o, in1=m_[:, 0:nq].unsqueeze(1).to_broadcast([128, 4, nq]), op=ALU.mult),
                                 reads=[pv, C_M1, C_M2], writes=[pv])
                        pts.append(pv)
                    return kbs, pts

                def finish(bi, g, kbs, pts):
                    nq = min(128, n - bi * 128)
                    db, ob_ = nbank(), nbank()
                    dn, oo = PS(db, 4 * nq), PS(ob_, 4 * nq)
                    L = len(pts)
                    for ki, pv in enumerate(pts):
                        P.op("pe", lambda e, o=dn.ap, r=pv.ap, st=(ki == 0), sp=(ki == L - 1): e.matmul(o, ones_bf[:, :], r, start=st, stop=sp),
                             reads=[pv, C_ONES], writes=[dn])
                    for ki, (kb, pv) in enumerate(zip(kbs, pts)):
                        vv = V(VV_[:, kb, g * 128:(g + 1) * 128], "shr")
                        P.op("pe", lambda e, o=oo.ap, l=vv.ap, r=pv.ap, st=(ki == 0), sp=(ki == L - 1): e.matmul(o, l, r, start=st, stop=sp),
                             reads=[vv, pv], writes=[oo])
                    ti_ = 4 + g % 2
                    rc = TMP(ti_, 4 * nq)
                    r3 = lambda ap: ap.rearrange("p (a b) -> p a b", a=4)
                    es_ap = esink[:, j * 16 + 4 * g:j * 16 + 4 * g + 4].unsqueeze(2).to_broadcast([128, 4, nq])
                    P.op("dve", lambda e, o=r3(tmp[:, ti_, 0:4 * nq]), a=r3(ps[:, db, 0:4 * nq]), s_=es_ap:
                         e.tensor_tensor(out=o, in0=a, in1=s_, op=ALU.add), reads=[dn, C_ES], writes=[rc])
                    if g != 3:
                        P.op("act", lambda e, o=rc.ap: e.activation(out=o, in_=o, func=AF.Ln), reads=[rc], writes=[rc])
                        P.op("act", lambda e, o=rc.ap: e.activation(out=o, in_=o, func=AF.Exp, scale=-1.0), reads=[rc], writes=[rc])
                    else:
                        P.op("dve", lambda e, o=rc.ap: e.reciprocal(out=o, in_=o), reads=[rc], writes=[rc])
                    ov = V(O_[:, 4 * g:4 * g + 4, bi * 128:bi * 128 + nq], "shr")
                    P.op("dve", lambda e, o=ov.ap, a=r3(ps[:, ob_, 0:4 * nq]), r=r3(tmp[:, ti_, 0:4 * nq]):
                         e.tensor_tensor(out=o, in0=a, in1=r, op=ALU.mult), reads=[oo, rc], writes=[ov])

                work = [(bi, g) for bi in range((n + 127) // 128) for g in range(4)]
                nxt = scores(*work[0])
                for wi, (bi, g) in enumerate(work):
                    cur = nxt
                    if wi + 1 < len(work):
                        nxt = scores(*work[wi + 1])
                    finish(bi, g, *cur)
                nbanks[0] = 6
                if qi + 1 < len(qt):
                    nxt = (0, layer, qt[qi + 1][0], qt[qi + 1][1])
                else:
                    nxt = (2, layer, FFN_T[layer][0][0], FFN_T[layer][0][1])
                norm_stats(*nxt)
                bank_s = out_proj(wo, O_, n, YS_low)
                post_norm_update(1, layer, t0, n, YS_low, bank_s, nxt, True)

        P.op("sp", lambda e: e.dma_start(out=gv[:, :], in_=gvd[:, :]), writes=[V(gv[:, :], "gv")], dma_sem=const_sem, group="const")
        P.op("sp", lambda e: e.dma_start(out=sink[:, :], in_=skd[:, :]), writes=[V(sink[:, :], "sink")], dma_sem=const_sem, group="const")
        P.op("act", lambda e: e.activation(out=esink[:, :], in_=sink[:, :], func=AF.Exp), reads=[V(sink[:, :], "sink")], writes=[V(esink[:, :], "esink")])
        t_ones, t_m = TMP(0, 128), TMP(1, 128)
        P.op("pool", lambda e: e.memset(tmp[:, 0, 0:128], 1.0), writes=[t_ones])
        P.op("dve", lambda e: e.tensor_copy(out=ones_bf[:, :], in_=tmp[:, 0, 0:128]), reads=[t_ones], writes=[V(ones_bf[:, :], "ones")])
        P.op("pool", lambda e: e.affine_select(out=tmp[:, 1, 0:128], in_=tmp[:, 0, 0:128], pattern=[[-1, 128]], compare_op=ALU.is_ge, fill=0.0, base=0, channel_multiplier=1),
             reads=[t_ones], writes=[t_m])
        P.op("dve", lambda e: e.tensor_copy(out=mask1[:, :], in_=tmp[:, 1, 0:128]), reads=[t_m], writes=[V(mask1[:, :], "mask1")])
        P.op("pool", lambda e: e.affine_select(out=tmp[:, 1, 0:128], in_=tmp[:, 0, 0:128], pattern=[[1, 128]], compare_op=ALU.is_ge, fill=0.0, base=0, channel_multiplier=-1),
             reads=[t_ones], writes=[t_m])
        P.op("dve", lambda e: e.tensor_copy(out=mask2[:, :], in_=tmp[:, 1, 0:128]), reads=[t_m], writes=[V(mask2[:, :], "mask2")])
        for u in range(NUNITS):
            for c0 in range(0, NCH, 4):
                P.op("sp", lambda e, o=xbuf[:, c0:c0 + 4, :], i=xd[u].rearrange("(c p) t -> p c t", p=128)[:, c0:c0 + 4, :]: e.dma_start(out=o, in_=i),
                     writes=[V(xbuf[:, c0:c0 + 4, :], "xbuf")], dma_sem=xload_sem[c0 // 4])
            P.op("sp", lambda e, i=cwd[u]: e.dma_start(out=cw[:, :], in_=i), writes=[V(cw[:, :], "cw")], dma_sem=cw_sem)
            for layer in range(4):
                if layer % 2 == 0:
                    conv_mixer(layer)
                else:
                    attn_mixer(layer, u)
                ffn(layer)
            for c0 in range(0, NCH, 4):
                P.op("sp", lambda e, i=xbuf[:, c0:c0 + 4, 0:NREAL], o=yd[u].rearrange("(c p) t -> p c t", p=128)[:, c0:c0 + 4, :]: e.dma_start(out=o, in_=i),
                     reads=[V(xbuf[:, c0:c0 + 4, 0:NREAL], "xbuf")], dma_sem=ystore_sem[c0 // 4])

        run = P.emit(nc, es)
        with nc.Block() as block:
            @block.tensor
            def _(t):
                run("pe", t)

            @block.scalar
            def _(a):
                run("act", a)

            @block.vector
            def _(v):
                run("dve", v)

            @block.gpsimd
            def _(g):
                run("pool", g)

            @block.sync
            def _(s):
                run("sp", s, finals=ystore_sem)
    return nc


_NC_CACHE = {}


def _unit_list():
    units = []
    for s in range(20):
        units.append((s, 0))
        units.append((s, 1))
    return units


def kernel(x_prompt, x_sample, conv_w_in, conv_w_dw, conv_w_out, attn_w_qkv, attn_w_o, attn_sink,
           ffn_w_gate, ffn_w_up, ffn_w_down, g_mix_pre, g_mix_post, g_ffn_pre, g_ffn_post):
    f32 = np.float32
    xs = np.concatenate([np.asarray(x_prompt, f32), np.asarray(x_sample, f32)], axis=0)
    units = _unit_list()
    inv_freq = (1.0 / (10000.0 ** (np.arange(0, 128, 2, dtype=f32) / f32(128)))).astype(f32)
    pos_l = np.arange(NT, dtype=f32)
    pos_r = (SEQ - 1 - np.arange(NT)).astype(f32)
    tabs = []
    for pos in (pos_l, pos_r):
        ang = (pos[:, None] * inv_freq[None, :]).astype(f32)
        c = np.cos(ang).astype(f32).T
        s = np.sin(ang).astype(f32).T
        tabs.append(np.stack([np.concatenate([c, c], 0), np.concatenate([-s, s], 0)], 0))
    cwt = []
    dw = np.asarray(conv_w_dw, f32)
    for rev in (0, 1):
        a = dw[:, ::-1, :] if rev else dw
        cwt.append(np.ascontiguousarray(a.reshape(2, 3, NCH, 128).transpose(3, 0, 1, 2).reshape(128, 96)))
    g_all = np.stack([np.asarray(g, f32) for g in (g_mix_pre, g_mix_post, g_ffn_pre, g_ffn_post)], 0)
    gv = np.ascontiguousarray(g_all.reshape(4, 4, NCH, 128).transpose(3, 0, 1, 2).reshape(128, 256))
    sink = np.ascontiguousarray(np.broadcast_to(np.asarray(attn_sink, f32).reshape(1, 32), (128, 32)))
    shared = {
        "gv_tab": gv, "sink_tab": sink,
        "conv_w_in": np.asarray(conv_w_in, f32), "conv_w_out": np.asarray(conv_w_out, f32),
        "attn_w_qkv": np.asarray(attn_w_qkv, f32), "attn_w_o": np.asarray(attn_w_o, f32),
        "ffn_w_gate": np.asarray(ffn_w_gate, f32), "ffn_w_up": np.asarray(ffn_w_up, f32),
        "ffn_w_down": np.asarray(ffn_w_down, f32),
    }
    in_maps = []
    for c in range(NCORES):
        xu = np.empty((NUNITS, D, NT), f32)
        cs = np.empty((NUNITS, 2, 128, NT), f32)
        cwu = np.empty((NUNITS, 128, 96), f32)
        for k in range(NUNITS):
            s, rev = units[c * NUNITS + k]
            seq = xs[s]
            if rev:
                seq = seq[::-1]
            xu[k] = seq[:NT].T
            cs[k] = tabs[rev]
            cwu[k] = cwt[rev]
        m = dict(shared)
        m.update({"x_units": xu, "cs_tab": cs, "cw_tab": cwu})
        in_maps.append(m)
    if "nc" not in _NC_CACHE:
        _NC_CACHE["nc"] = build_nc()
    res = run_bass_kernel_spmd(_NC_CACHE["nc"], in_maps, core_ids=list(range(NCORES)))
    out = np.empty((20, SEQ, D), f32)
    for c in range(NCORES):
        yu = res.results[c]["y_units"]
        for k in range(NUNITS):
            s, rev = units[c * NUNITS + k]
            blk = yu[k].T
            if rev:
                out[s, SEQ - NREAL:] = blk[::-1]
            else:
                out[s, :NREAL] = blk
    return out[:16].copy(), out[16:].copy()
```
